# Optimizing a Trainium2 kernel written in Bass

```python
import jax, jax.numpy as jnp
from jax import lax
import numpy as np

D_MODEL = 1024
BATCH = 8
SEQ = 8192
DEPTH = 2

D_MIX = D_MODEL
D_A = D_MIX // 4
D_B = D_MIX // 4
D_C = D_MIX // 2
A_HEADS = 4
A_HEAD_DIM = D_A // A_HEADS
SGU_CHUNK = 128
CONV_WIDTH = 31
C_HEADS = 4
C_HEAD_DIM = D_C // C_HEADS
HGRN_CHUNK = 64
LN_EPS = 1e-5
DEEPNORM_ALPHA = (2 * DEPTH) ** 0.25
DEEPNORM_BETA = (8 * DEPTH) ** -0.25
SPLIT_SIZES = (D_A, D_A, D_A, D_B, D_B, D_B, D_C, D_C, D_C, D_C)
D_IN = 3 * D_A + 3 * D_B + 4 * D_C

kernel_name = "hybrid_sgu_conformer_hgrn2_deepnorm"


def _layernorm(x, g, b):
    xf = x.astype(jnp.float32)
    mu = jnp.mean(xf, axis=-1, keepdims=True)
    var = jnp.mean(jnp.square(xf - mu), axis=-1, keepdims=True)
    return ((xf - mu) * lax.rsqrt(var + LN_EPS)).astype(x.dtype) * g + b


def _sgu_mixer(u, v, ln_g, ln_b, w_s, b_s):
    u = jax.nn.gelu(u)
    v = _layernorm(jax.nn.gelu(v), ln_g, ln_b)
    bsz, seq, _ = v.shape
    n_chunks = seq // SGU_CHUNK
    vc = v.reshape(bsz, n_chunks, SGU_CHUNK, A_HEADS, A_HEAD_DIM)
    causal = jnp.tril(jnp.ones((SGU_CHUNK, SGU_CHUNK), dtype=bool))
    w = jnp.where(causal[None], w_s, 0)
    mixed = jnp.einsum('hts,bnshd->bnthd', w, vc) + b_s.T[None, None, :, :, None]
    return u * mixed.reshape(bsz, seq, D_A)


def _conv_module(val, glu_gate, w_dw, b_dw, ln_g, ln_b):
    h = val * jax.nn.sigmoid(glu_gate)
    h = lax.conv_general_dilated(
        h, w_dw[:, None, :], window_strides=(1,),
        padding=[(CONV_WIDTH - 1, 0)],
        dimension_numbers=('NWC', 'WIO', 'NWC'),
        feature_group_count=D_B) + b_dw
    h = _layernorm(h, ln_g, ln_b)
    return jax.nn.silu(h)


def _hgrn2_mixer(q, f_logit, i, lb, norm_g):
    bsz, seq, _ = q.shape
    out_dtype = q.dtype
    f = lb + (1.0 - lb) * jax.nn.sigmoid(f_logit.astype(jnp.float32))
    log_f = jnp.log(f)
    k = 1.0 - f
    qf = jax.nn.silu(q.astype(jnp.float32))
    vf = i.astype(jnp.float32)
    n_chunks = seq // HGRN_CHUNK

    def to_chunks(t):
        return t.reshape(bsz, n_chunks, HGRN_CHUNK, C_HEADS, C_HEAD_DIM).transpose(1, 0, 3, 2, 4)

    causal = jnp.tril(jnp.ones((HGRN_CHUNK, HGRN_CHUNK), dtype=bool))

    def step(state, inp):
        qc, kc, vc, gc = inp
        b = jnp.cumsum(gc, axis=2)
        diff = b[:, :, :, None, :] - b[:, :, None, :, :]
        decay = jnp.exp(jnp.where(causal[None, None, :, :, None], diff, -jnp.inf))
        scores = jnp.einsum('bhtd,bhsd,bhtsd->bhts', qc, kc, decay)
        o_intra = jnp.einsum('bhts,bhsv->bhtv', scores, vc)
        o_inter = jnp.einsum('bhtd,bhdv->bhtv', qc * jnp.exp(b), state)
        b_last = b[:, :, -1:, :]
        new_state = (jnp.exp(b_last[:, :, 0, :])[..., None] * state
                     + jnp.einsum('bhsd,bhsv->bhdv', kc * jnp.exp(b_last - b), vc))
        return new_state, o_intra + o_inter

    state0 = jnp.zeros((bsz, C_HEADS, C_HEAD_DIM, C_HEAD_DIM), jnp.float32)
    _, o = lax.scan(step, state0, (to_chunks(qf), to_chunks(k), to_chunks(vf), to_chunks(log_f)))
    o = o.transpose(1, 0, 3, 2, 4).reshape(bsz, seq, C_HEADS, C_HEAD_DIM)
    o = o * lax.rsqrt(jnp.mean(jnp.square(o), axis=-1, keepdims=True) + LN_EPS)
    return o.reshape(bsz, seq, D_C).astype(out_dtype) * norm_g


def _hybrid_layer(x, w_in, sgu_ln_g, sgu_ln_b, sgu_w, sgu_b, conv_w, conv_b,
                  conv_ln_g, conv_ln_b, lb, hgrn_norm_g, w_out, post_ln_g, post_ln_b):
    h = jnp.einsum('bsd,de->bse', x, w_in)
    offsets = np.cumsum(SPLIT_SIZES)[:-1].tolist()
    a_u, a_v, a_g, b_val, b_glu, b_g, c_q, c_f, c_i, c_g = jnp.split(h, offsets, axis=-1)
    y_a = _sgu_mixer(a_u, a_v, sgu_ln_g, sgu_ln_b, sgu_w, sgu_b) * jax.nn.silu(a_g)
    y_b = _conv_module(b_val, b_glu, conv_w, conv_b, conv_ln_g, conv_ln_b) * jax.nn.silu(b_g)
    y_c = _hgrn2_mixer(c_q, c_f, c_i, lb, hgrn_norm_g) * jax.nn.silu(c_g)
    y = jnp.concatenate([y_a, y_b, y_c], axis=-1)
    y = jnp.einsum('bse,ed->bsd', y, w_out)
    return _layernorm(DEEPNORM_ALPHA * x + y, post_ln_g, post_ln_b)


def setup_inputs(seed: int = 0) -> dict:
    key = jax.random.key(seed)
    ks = jax.random.split(key, 16)
    f32 = jnp.float32
    nrm = lambda k, shape, s: jax.random.normal(k, shape, f32) * s
    x = jax.random.normal(ks[0], (BATCH, SEQ, D_MODEL), f32)
    w_in = nrm(ks[1], (DEPTH, D_MODEL, D_IN), D_MODEL ** -0.5)
    sgu_ln_g = 1.0 + nrm(ks[2], (DEPTH, D_A), 0.02)
    sgu_ln_b = nrm(ks[3], (DEPTH, D_A), 0.02)
    sgu_w = nrm(ks[4], (DEPTH, A_HEADS, SGU_CHUNK, SGU_CHUNK), 0.5 * SGU_CHUNK ** -0.5)
    sgu_b = 1.0 + nrm(ks[5], (DEPTH, A_HEADS, SGU_CHUNK), 0.02)
    conv_w = nrm(ks[6], (DEPTH, CONV_WIDTH, D_B), CONV_WIDTH ** -0.5)
    conv_b = nrm(ks[7], (DEPTH, D_B), 0.01)
    conv_ln_g = 1.0 + nrm(ks[8], (DEPTH, D_B), 0.02)
    conv_ln_b = nrm(ks[9], (DEPTH, D_B), 0.02)
    hgrn_lb = nrm(ks[10], (DEPTH, D_C), 0.1)
    hgrn_norm_g = 1.0 + nrm(ks[11], (DEPTH, D_C), 0.02)
    w_out = nrm(ks[12], (DEPTH, D_MIX, D_MODEL), DEEPNORM_BETA * D_MIX ** -0.5)
    post_ln_g = 1.0 + nrm(ks[13], (DEPTH, D_MODEL), 0.02)
    post_ln_b = nrm(ks[14], (DEPTH, D_MODEL), 0.02)
    return {"x": x, "w_in": w_in, "sgu_ln_g": sgu_ln_g, "sgu_ln_b": sgu_ln_b,
            "sgu_w": sgu_w, "sgu_b": sgu_b, "conv_w": conv_w, "conv_b": conv_b,
            "conv_ln_g": conv_ln_g, "conv_ln_b": conv_ln_b, "hgrn_lb": hgrn_lb,
            "hgrn_norm_g": hgrn_norm_g, "w_out": w_out, "post_ln_g": post_ln_g,
            "post_ln_b": post_ln_b}


def reference(x, w_in, sgu_ln_g, sgu_ln_b, sgu_w, sgu_b, conv_w, conv_b, conv_ln_g,
              conv_ln_b, hgrn_lb, hgrn_norm_g, w_out, post_ln_g, post_ln_b):
    gamma = jax.nn.softmax(hgrn_lb.astype(jnp.float32), axis=0)
    lower_bounds = jnp.cumsum(gamma, axis=0) - gamma[0:1]
    for l in range(DEPTH):
        x = _hybrid_layer(x, w_in[l], sgu_ln_g[l], sgu_ln_b[l], sgu_w[l], sgu_b[l],
                          conv_w[l], conv_b[l], conv_ln_g[l], conv_ln_b[l],
                          lower_bounds[l], hgrn_norm_g[l], w_out[l], post_ln_g[l], post_ln_b[l])
    return x
```

```python
import contextlib
import numpy as np
import concourse.bass as bass
import concourse.mybir as mybir
from concourse.bass_utils import run_bass_kernel_spmd

F32 = mybir.dt.float32
BF16 = mybir.dt.bfloat16
AF = mybir.ActivationFunctionType
ALU = mybir.AluOpType

D_MODEL = 1024
BATCH = 8
SEQ = 8192
DEPTH = 2
D_A = 256
D_B = 256
D_C = 512
D_IN = 3584
CONV_W = 31
HALO = CONV_W - 1
LN_EPS = 1e-5
ALPHA = (2 * DEPTH) ** 0.25
P = 128


class Buf:
    __slots__ = ("name", "last_write", "readers", "excl", "dma_sem", "dma_count")

    def __init__(self, name, excl=False):
        self.name = name
        self.last_write = None
        self.readers = []
        self.excl = excl
        self.dma_sem = None
        self.dma_count = 0


class Op:
    __slots__ = ("eng", "fn", "waits", "idx", "inc", "dma_tok")

    def __init__(self, eng, fn):
        self.eng = eng
        self.fn = fn
        self.waits = []
        self.idx = None
        self.inc = False
        self.dma_tok = None


class Sched:
    ENGS = ("pe", "act", "dve", "pool", "sp")

    def __init__(self, nc, es):
        self.nc = nc
        self.es = es
        self.handles = {"pe": nc.tensor, "act": nc.scalar, "dve": nc.vector, "pool": nc.gpsimd,
                        "sp": nc.sync}
        self.ops = {e: [] for e in self.ENGS}
        self.order = []
        self.sems = {e: es.enter_context(nc.semaphore("s_" + e)) for e in self.ENGS}
        self.nbuf = 0

    def buf(self, name, excl=False):
        return Buf(name, excl)

    def dma_buf(self, name):
        b = Buf(name)
        b.dma_sem = self.es.enter_context(self.nc.semaphore("d_" + name))
        return b

    def _deps(self, reads, writes):
        deps = []
        for b in reads:
            if b.excl:
                writes = list(writes) + [b]
                continue
            if b.last_write is not None:
                deps.append(b.last_write)
        for b in writes:
            if b.last_write is not None:
                deps.append(b.last_write)
            deps.extend(b.readers)
        return deps, writes

    def op(self, eng, fn, reads=(), writes=()):
        o = Op(eng, fn)
        deps, writes = self._deps(reads, writes)
        o.waits = deps
        o.idx = len(self.ops[eng])
        self.ops[eng].append(o)
        self.order.append(o)
        tok = ("e", eng, o.idx)
        for b in reads:
            if not b.excl:
                b.readers.append(tok)
        for b in writes:
            b.last_write = tok
            b.readers = []
        return o

    def dma(self, eng, fn, sbuf_side, reads=(), writes=()):
        o = Op(eng, fn)
        deps, writes = self._deps(reads, writes)
        o.waits = deps
        o.idx = len(self.ops[eng])
        sbuf_side.dma_count += 16
        o.dma_tok = (sbuf_side.dma_sem, sbuf_side.dma_count)
        self.ops[eng].append(o)
        self.order.append(o)
        tok = ("d", sbuf_side.dma_sem, sbuf_side.dma_count)
        for b in reads:
            if not b.excl:
                b.readers.append(tok)
        for b in writes:
            b.last_write = tok
            b.readers = []
        return o

    def emit(self, final_wait_bufs=()):
        for o in self.order:
            for t in o.waits:
                if t[0] == "e":
                    if t[1] == o.eng and o.eng in ("pe", "sp"):
                        continue
                    self.ops[t[1]][t[2]].inc = True
        val = {}
        for e in self.ENGS:
            c = 0
            vals = []
            for o in self.ops[e]:
                if o.inc:
                    c += 1
                vals.append(c)
            val[e] = vals
        known = {e: {} for e in self.ENGS}
        nwait = 0
        for o in self.order:
            h = self.handles[o.eng]
            kn = known[o.eng]
            need = {}
            for t in o.waits:
                if t[0] == "e":
                    if t[1] == o.eng and o.eng in ("pe", "sp"):
                        continue
                    key = ("e", t[1])
                    sem = self.sems[t[1]]
                    v = val[t[1]][t[2]]
                else:
                    key = ("d", id(t[1]))
                    sem = t[1]
                    v = t[2]
                if kn.get(key, 0) >= v:
                    continue
                if key not in need or need[key][1] < v:
                    need[key] = (sem, v)
            for key, (sem, v) in need.items():
                h.wait_ge(sem, v)
                kn[key] = v
                nwait += 1
            ins = o.fn(h)
            if o.dma_tok is not None:
                ins.then_inc(o.dma_tok[0], 16)
            elif o.inc:
                ins.then_inc(self.sems[o.eng], 1)
        for b in final_wait_bufs:
            self.nc.sync.wait_ge(b.dma_sem, b.dma_count)
        return nwait


def build_program(S, layers, T=256):
    assert S % T == 0 and T % 128 == 0
    NS = T // 128
    NT = S // T
    NCH = T // 64
    nc = bass.Bass("TRN2", target_bir_lowering=False)
    L = len(layers)

    def din(name, shape):
        return nc.dram_tensor(name, list(shape), F32, kind="ExternalInput").ap()

    x_d = din("x", (S, D_MODEL))
    w_in_d = din("w_in", (DEPTH, D_MODEL, D_IN))
    sgu_ln_g_d = din("sgu_ln_g", (DEPTH, D_A))
    sgu_ln_b_d = din("sgu_ln_b", (DEPTH, D_A))
    sgu_w_d = din("sgu_w", (DEPTH, 4, 128, 128))
    sgu_b_d = din("sgu_b", (DEPTH, 4, 128))
    conv_w_d = din("conv_w", (DEPTH, CONV_W, D_B))
    conv_b_d = din("conv_b", (DEPTH, D_B))
    conv_ln_g_d = din("conv_ln_g", (DEPTH, D_B))
    conv_ln_b_d = din("conv_ln_b", (DEPTH, D_B))
    hgrn_lb_d = din("hgrn_lb", (DEPTH, D_C))
    hgrn_norm_g_d = din("hgrn_norm_g", (DEPTH, D_C))
    w_out_d = din("w_out", (DEPTH, D_MODEL, D_MODEL))
    post_ln_g_d = din("post_ln_g", (DEPTH, D_MODEL))
    post_ln_b_d = din("post_ln_b", (DEPTH, D_MODEL))
    out_d = nc.dram_tensor("out", [S, D_MODEL], F32, kind="ExternalOutput").ap()
    scratch_d = None
    if L > 1:
        scratch_d = nc.dram_tensor("xmid", [S, D_MODEL], F32, kind="Internal").ap()

    es = contextlib.ExitStack()
    with es:
        sc = Sched(nc, es)

        def sb(name, shape, dt=F32):
            return es.enter_context(nc.sbuf_tensor(name, list(shape), dt))

        def ps(name, shape, dt=F32):
            return es.enter_context(nc.psum_tensor(name, list(shape), dt))

        w_tok = sb("w_tok", [P, 8, 2048], BF16)
        w_feat = sb("w_feat", [P, 8, 1536], BF16)
        w_o = sb("w_o", [P, 8, 1024], BF16)
        stage = [sb("stage%d" % i, [P, 896], F32) for i in range(2)]
        identf = sb("identf", [P, P], F32)
        identb = sb("identb", [P, P], BF16)
        mask4 = sb("mask4", [P, 4, P], F32)
        rmask = sb("rmask", [P, T], F32)
        nhalf = sb("nhalf", [P, 8], F32)
        WsT = sb("WsT", [P, 4, P], BF16)
        wstg = sb("wstg", [P, 4, P], F32)
        bs_t = sb("bs_t", [P, 4], F32)
        small_stg = sb("small_stg", [32, 512], F32)
        cwT = sb("cwT", [P, 2, CONV_W], F32)
        diag = sb("diag", [P, 2, CONV_W, P], BF16)
        lbT = sb("lbT", [P, 8], F32)
        lbtmp = sb("lbtmp", [P, 16], F32)
        hc_a = sb("hc_a", [P, 4], F32)
        hc_c = sb("hc_c", [P, 4], F32)
        hc_nc = sb("hc_nc", [P, 4], F32)
        bc_sgu_g = sb("bc_sgu_g", [P, D_A], F32)
        bc_sgu_b = sb("bc_sgu_b", [P, D_A], F32)
        bc_conv_b = sb("bc_conv_b", [P, D_B], F32)
        bc_conv_g = sb("bc_conv_g", [P, D_B], F32)
        bc_conv_lb = sb("bc_conv_lb", [P, D_B], F32)
        bc_ng = sb("bc_ng", [P, D_C], F32)
        bc_pg = sb("bc_pg", [P, D_MODEL], F32)
        bc_pb = sb("bc_pb", [P, D_MODEL], F32)

        NXS = 2 * NS
        xin = [sb("xin%d" % i, [P, D_MODEL], F32) for i in range(NXS)]
        xb = [sb("xb%d" % i, [P, D_MODEL], BF16) for i in range(2)]
        xT = sb("xT", [P, 8, T], BF16)
        uv = [sb("uv%d" % i, [P, 512], F32) for i in range(NS)]
        sgab = [sb("sgab%d" % i, [P, 512], F32) for i in range(NS)]
        vtok = [sb("vtok%d" % i, [P, 512], BF16) for i in range(NS)]
        sgc = [sb("sgc%d" % i, [P, 512], F32) for i in range(NS)]
        vn = sb("vn", [P, D_A], F32)
        vbf = sb("vbf", [P, D_A], BF16)
        usg = sb("usg", [P, D_A], F32)
        stats = sb("stats", [P, 12], F32)
        mv = sb("mv", [P, 2], F32)
        rstd = sb("rstd", [P, 4], F32)
        nmr = sb("nmr", [P, 1], F32)
        hT = [sb("hT%d" % i, [P, 2, HALO + T], BF16) for i in range(2)]
        thb = sb("thb", [P, T], F32)
        sq_t = sb("sq_t", [P, T], F32)
        thf = sb("thf", [P, T], F32)
        g_t = sb("g_t", [P, T], F32)
        kk = sb("kk", [P, T], F32)
        b_t = sb("b_t", [P, T], F32)
        eb = sb("eb", [P, T], F32)
        enb = sb("enb", [P, T], F32)
        qT = sb("qT", [P, 4, T], BF16)
        kT = sb("kT", [P, 4, T], BF16)
        aj = sb("aj", [P, 4, NCH], F32)
        cv = sb("cv", [P, D_B], F32)
        cn = sb("cn", [P, D_B], F32)
        cs = sb("cs", [P, D_B], F32)
        ktok = sb("ktok", [P, 512], BF16)
        scm = sb("scm", [P, 4, P], BF16)
        Dp = sb("Dp", [P, 512], F32)
        Sst = sb("Sst", [P, 512], F32)
        Sbf = [sb("Sbf%d" % i, [P, 512], BF16) for i in range(2)]
        ngs = sb("ngs", [P, 512], F32)
        ss = sb("ss", [P, 4], F32)
        junk = sb("junk", [P, P], F32)
        ybf = sb("ybf", [P, D_MODEL], BF16)
        yT = sb("yT", [P, 8, P], BF16)

        tokP = [ps("tokP%d" % i, [P, 512]) for i in range(2)]
        featP = [ps("featP%d" % i, [P, 512]) for i in range(2)]
        trP = ps("trP", [P, 1024], BF16)
        mixP = ps("mixP", [P, 512])
        hgP = ps("hgP", [P, 512])
        oP = ps("oP", [P, 512])

        B = {}

        def mk(name, excl=False):
            B[name] = sc.buf(name, excl)
            return B[name]

        for n in ["w_tok", "w_feat", "w_o", "identf", "identb", "mask4", "rmask", "nhalf", "WsT", "bs_t",
                  "cwT", "diag", "lbT", "lbtmp", "hc", "xT", "vn", "vbf", "usg", "stats", "mv", "rstd", "nmr",
                  "thb", "sq_t", "thf", "g_t", "kk", "b_t", "eb", "enb", "qT", "kT", "aj", "cv", "cn", "cs",
                  "ktok", "scm", "Dp", "Sst", "ngs", "ss", "junk", "ybf", "yT", "xb0", "xb1", "hT0", "hT1",
                  "Sbf0", "Sbf1"]:
            mk(n)
        for i in range(NS):
            for n in ["uv", "sgab", "vtok", "sgc"]:
                mk("%s%d" % (n, i))
        for n in ["tokP0", "tokP1", "featP0", "featP1", "trP", "mixP", "hgP", "oP"]:
            mk(n, excl=True)
        Bstage = [sc.dma_buf("stage%d" % i) for i in range(2)]
        Bxin = [sc.dma_buf("xin%d" % i) for i in range(NXS)]
        Bconst = sc.dma_buf("const")
        Bwstg = sc.dma_buf("wstg")
        Bsmall = sc.dma_buf("small_stg")
        bc_bufs = {}
        for n in ["bc_sgu_g", "bc_sgu_b", "bc_conv_b", "bc_conv_g", "bc_conv_lb", "bc_ng", "bc_pg", "bc_pb"]:
            bc_bufs[n] = sc.dma_buf(n)
        Bmid = [sc.buf("mid%d" % i) for i in range(S // 128)]

        sc.op("pool", lambda e: e.memset(identf[:], 1.0), writes=[B["identf"]])
        sc.op("pool", lambda e: e.affine_select(out=identf[:], in_=identf[:], pattern=[[-1, P]],
                                                 compare_op=ALU.is_equal, fill=0.0, base=0,
                                                 channel_multiplier=1),
              reads=[B["identf"]], writes=[B["identf"]])
        sc.op("dve", lambda e: e.tensor_copy(out=identb[:], in_=identf[:]), reads=[B["identf"]],
              writes=[B["identb"]])
        sc.op("pool", lambda e: e.memset(mask4[:], 1.0), writes=[B["mask4"]])
        sc.op("pool", lambda e: e.affine_select(out=mask4[:], in_=mask4[:], pattern=[[0, 4], [1, P]],
                                                 compare_op=ALU.is_ge, fill=0.0, base=0,
                                                 channel_multiplier=-1),
              reads=[B["mask4"]], writes=[B["mask4"]])
        sc.op("pool", lambda e: e.memset(mask4[0:64, :, 64:128], 0.0), reads=[B["mask4"]],
              writes=[B["mask4"]])
        sc.op("pool", lambda e: e.memset(rmask[:], 1.0), writes=[B["rmask"]])
        sc.op("pool", lambda e: e.memset(rmask[:].rearrange("p (c j) -> p c j", j=64)[:, :, 0:1], 0.0),
              reads=[B["rmask"]], writes=[B["rmask"]])
        sc.op("pool", lambda e: e.memset(nhalf[:], -0.5), writes=[B["nhalf"]])

        cast_rr = [0]

        def cast_copy(out_ap, in_ap, reads, writes):
            k = cast_rr[0] % 3
            cast_rr[0] += 1
            if k == 0:
                sc.op("act", lambda e: e.copy(out=out_ap, in_=in_ap), reads=reads, writes=writes)
            elif k == 1:
                sc.op("dve", lambda e: e.tensor_copy(out=out_ap, in_=in_ap), reads=reads, writes=writes)
            else:
                sc.op("pool", lambda e: e.tensor_copy(out=out_ap, in_=in_ap), reads=reads, writes=writes)

        def rstd_from_var(var_ap, out_ap, n, reads, eps=LN_EPS, scale=1.0):
            sc.op("dve", lambda e: e.tensor_scalar(out=out_ap, in0=var_ap, scalar1=scale, scalar2=eps,
                                                   op0=ALU.mult, op1=ALU.add),
                  reads=reads, writes=[B["rstd"]])
            sc.op("pool", lambda e: e.tensor_tensor(out=out_ap, in0=out_ap, in1=nhalf[:, 0:n], op=ALU.pow),
                  reads=[B["rstd"], B["nhalf"]], writes=[B["rstd"]])

        def layer_setup(l):
            pieces = [
                [(0, 768, "tok", 0), (768, 128, "feat", 0)],
                [(0, 384, "feat", 128), (384, 256, "tok", 768), (640, 256, "feat", 512)],
                [(0, 768, "feat", 768), (768, 128, "tok", 1024)],
                [(0, 896, "tok", 1152)],
            ]
            si = 0
            for kc in range(8):
                for q in range(4):
                    st = stage[si % 2]
                    bst = Bstage[si % 2]
                    si += 1
                    src = w_in_d[l, kc * 128:(kc + 1) * 128, q * 896:(q + 1) * 896]
                    sc.dma("sp", lambda e, st=st, src=src: e.dma_start(out=st[:], in_=src), bst, writes=[bst])
                    for (s0, wd, which, d0) in pieces[q]:
                        dst = w_tok if which == "tok" else w_feat
                        cast_copy(dst[:, kc, d0:d0 + wd], st[:, s0:s0 + wd], [bst],
                                  [B["w_tok"] if which == "tok" else B["w_feat"]])
            for kc in range(8):
                for hf in range(2):
                    st = stage[si % 2]
                    bst = Bstage[si % 2]
                    si += 1
                    src = w_out_d[l, kc * 128:(kc + 1) * 128, hf * 512:(hf + 1) * 512]
                    sc.dma("sp", lambda e, st=st, src=src: e.dma_start(out=st[:, 0:512], in_=src), bst,
                           writes=[bst])
                    cast_copy(w_o[:, kc, hf * 512:(hf + 1) * 512], st[:, 0:512], [bst], [B["w_o"]])

            def bc(name, tile, src_row):
                b = bc_bufs[name]
                sc.dma("sp", lambda e: e.dma_start(out=tile[:], in_=src_row.partition_broadcast(P)), b,
                       writes=[b])
            bc("bc_sgu_g", bc_sgu_g, sgu_ln_g_d[l:l + 1, :])
            bc("bc_sgu_b", bc_sgu_b, sgu_ln_b_d[l:l + 1, :])
            bc("bc_conv_b", bc_conv_b, conv_b_d[l:l + 1, :])
            bc("bc_conv_g", bc_conv_g, conv_ln_g_d[l:l + 1, :])
            bc("bc_conv_lb", bc_conv_lb, conv_ln_b_d[l:l + 1, :])
            bc("bc_ng", bc_ng, hgrn_norm_g_d[l:l + 1, :])
            bc("bc_pg", bc_pg, post_ln_g_d[l:l + 1, :])
            bc("bc_pb", bc_pb, post_ln_b_d[l:l + 1, :])

            sc.dma("sp", lambda e: e.dma_start(out=wstg[:], in_=sgu_w_d[l].rearrange("h t s -> t h s")),
                   Bwstg, writes=[Bwstg])
            sc.op("pool", lambda e: e.affine_select(out=wstg[:], in_=wstg[:], pattern=[[0, 4], [-1, P]],
                                                     compare_op=ALU.is_ge, fill=0.0, base=0,
                                                     channel_multiplier=1),
                  reads=[Bwstg], writes=[Bwstg])
            for h in range(4):
                sc.op("pe", lambda e, h=h: e.transpose(out=mixP[:, h * 128:(h + 1) * 128], in_=wstg[:, h, :],
                                                       identity=identf[:]),
                      reads=[Bwstg, B["identf"]], writes=[B["mixP"]])
            sc.op("dve", lambda e: e.tensor_copy(out=WsT[:].rearrange("p h t -> p (h t)"), in_=mixP[:]),
                  reads=[B["mixP"]], writes=[B["WsT"]])

            sc.dma("sp", lambda e: e.dma_start(out=small_stg[0:4, 0:128], in_=sgu_b_d[l]), Bsmall,
                   writes=[Bsmall])
            sc.dma("sp", lambda e: e.dma_start(out=small_stg[0:31, 128:384], in_=conv_w_d[l]), Bsmall,
                   writes=[Bsmall])
            sc.dma("sp", lambda e: e.dma_start(out=small_stg[0:8, 384:512],
                                               in_=hgrn_lb_d.rearrange("l (h d) -> (l h) d", d=128)),
                   Bsmall, writes=[Bsmall])
            sc.op("pe", lambda e: e.transpose(out=hgP[:, 0:4], in_=small_stg[0:4, 0:128],
                                              identity=identf[0:4, 0:4]),
                  reads=[Bsmall, B["identf"]], writes=[B["hgP"]])
            for cc in range(2):
                sc.op("pe", lambda e, cc=cc: e.transpose(out=hgP[:, 32 + cc * 32:32 + cc * 32 + 31],
                                                         in_=small_stg[0:31, 128 + cc * 128:256 + cc * 128],
                                                         identity=identf[0:31, 0:31]),
                      reads=[Bsmall, B["identf"]], writes=[B["hgP"]])
            sc.op("pe", lambda e: e.transpose(out=hgP[:, 96:104], in_=small_stg[0:8, 384:512],
                                              identity=identf[0:8, 0:8]),
                  reads=[Bsmall, B["identf"]], writes=[B["hgP"]])
            sc.op("dve", lambda e: e.tensor_copy(out=bs_t[:], in_=hgP[:, 0:4]), reads=[B["hgP"]],
                  writes=[B["bs_t"]])
            sc.op("dve", lambda e: e.tensor_copy(out=cwT[:], in_=hgP[:, 32:96].rearrange(
                "p (c k) -> p c k", k=32)[:, :, 0:31]), reads=[B["hgP"]], writes=[B["cwT"]])
            sc.op("dve", lambda e: e.tensor_copy(out=lbT[:], in_=hgP[:, 96:104]), reads=[B["hgP"]],
                  writes=[B["lbT"]])
            for cc in range(2):
                for k in range(CONV_W):
                    sc.op("dve", lambda e, cc=cc, k=k: e.tensor_scalar(
                        out=diag[:, cc, k, :], in0=identf[:], scalar1=cwT[:, cc, k:k + 1], scalar2=0.5,
                        op0=ALU.mult, op1=ALU.mult), reads=[B["identf"], B["cwT"]], writes=[B["diag"]])
            sc.op("act", lambda e: e.activation(out=lbtmp[:, 0:8], in_=lbT[:], func=AF.Exp),
                  reads=[B["lbT"]], writes=[B["lbtmp"]])
            sc.op("dve", lambda e: e.tensor_tensor(out=lbtmp[:, 8:12], in0=lbtmp[:, 0:4], in1=lbtmp[:, 4:8],
                                                   op=ALU.add), reads=[B["lbtmp"]], writes=[B["lbtmp"]])
            sc.op("dve", lambda e: e.reciprocal(out=lbtmp[:, 12:16], in_=lbtmp[:, 8:12]),
                  reads=[B["lbtmp"]], writes=[B["lbtmp"]])
            sc.op("dve", lambda e: e.tensor_tensor(out=lbtmp[:, 0:4], in0=lbtmp[:, 0:4], in1=lbtmp[:, 12:16],
                                                   op=ALU.mult), reads=[B["lbtmp"]], writes=[B["lbtmp"]])
            sc.op("dve", lambda e: e.tensor_tensor(out=lbtmp[:, 4:8], in0=lbtmp[:, 4:8], in1=lbtmp[:, 12:16],
                                                   op=ALU.mult), reads=[B["lbtmp"]], writes=[B["lbtmp"]])
            if l == 0:
                sc.op("dve", lambda e: e.tensor_tensor(out=lbtmp[:, 8:12], in0=lbtmp[:, 0:4],
                                                       in1=lbtmp[:, 0:4], op=ALU.subtract),
                      reads=[B["lbtmp"]], writes=[B["lbtmp"]])
            else:
                sc.op("dve", lambda e: e.tensor_tensor(out=lbtmp[:, 8:12], in0=lbtmp[:, 0:4],
                                                       in1=lbtmp[:, 4:8], op=ALU.add),
                      reads=[B["lbtmp"]], writes=[B["lbtmp"]])
                sc.op("dve", lambda e: e.tensor_tensor(out=lbtmp[:, 8:12], in0=lbtmp[:, 8:12],
                                                       in1=lbtmp[:, 0:4], op=ALU.subtract),
                      reads=[B["lbtmp"]], writes=[B["lbtmp"]])
            sc.op("dve", lambda e: e.tensor_scalar(out=hc_a[:], in0=lbtmp[:, 8:12], scalar1=0.5, scalar2=0.5,
                                                   op0=ALU.mult, op1=ALU.add),
                  reads=[B["lbtmp"]], writes=[B["hc"]])
            sc.op("dve", lambda e: e.tensor_scalar(out=hc_c[:], in0=lbtmp[:, 8:12], scalar1=-0.5, scalar2=0.5,
                                                   op0=ALU.mult, op1=ALU.add),
                  reads=[B["lbtmp"]], writes=[B["hc"]])
            sc.op("dve", lambda e: e.tensor_scalar(out=hc_nc[:], in0=lbtmp[:, 8:12], scalar1=0.5, scalar2=-0.5,
                                                   op0=ALU.mult, op1=ALU.add),
                  reads=[B["lbtmp"]], writes=[B["hc"]])
            sc.op("pool", lambda e: e.memset(Sst[:], 0.0), writes=[B["Sst"]])
            cur0 = state["sbf"]
            sc.op("pool", lambda e: e.memset(Sbf[cur0][:], 0.0), writes=[B["Sbf%d" % cur0]])
            sc.op("pool", lambda e: e.memset(hT[1][:, :, T:T + HALO], 0.0), writes=[B["hT1"]])

        state = {"pp": 0, "fp": 0, "sbf": 0}

        def tok_bank():
            i = state["pp"]
            state["pp"] ^= 1
            return tokP[i], B["tokP%d" % i]

        def feat_bank():
            i = state["fp"]
            state["fp"] ^= 1
            return featP[i], B["featP%d" % i]

        def load_tile(src_d, ti, slot, src_bufs):
            for s in range(NS):
                r0 = ti * T + s * 128
                xi = xin[slot * NS + s]
                bx = Bxin[slot * NS + s]
                rd = [src_bufs[r0 // 128]] if src_bufs is not None else []
                sc.dma("sp", lambda e, xi=xi, r0=r0: e.dma_start(out=xi[:], in_=src_d[r0:r0 + 128, :]), bx,
                       reads=rd, writes=[bx])

        def tile_body(l, ti, slot, dst_d, dst_bufs):
            hcur = ti % 2
            hprev = 1 - hcur
            hTc, hTp = hT[hcur], hT[hprev]
            BhTc, BhTp = B["hT%d" % hcur], B["hT%d" % hprev]
            sc.op("pool", lambda e: e.tensor_copy(out=hTc[:, :, 0:HALO], in_=hTp[:, :, T:T + HALO]),
                  reads=[BhTp], writes=[BhTc])
            for s in range(NS):
                xi = xin[slot * NS + s]
                bx = Bxin[slot * NS + s]
                xbs, Bxb = xb[s % 2], B["xb%d" % (s % 2)]
                sc.op("pool", lambda e, xbs=xbs, xi=xi: e.tensor_copy(out=xbs[:], in_=xi[:]), reads=[bx],
                      writes=[Bxb])
                for kc in range(8):
                    sc.op("pe", lambda e, kc=kc, xbs=xbs: e.transpose(
                        out=trP[:, kc * 128:(kc + 1) * 128], in_=xbs[:, kc * 128:(kc + 1) * 128],
                        identity=identb[:]), reads=[Bxb, B["identb"]], writes=[B["trP"]])
                sc.op("dve", lambda e, s=s: e.tensor_copy(
                    out=xT[:, :, s * 128:(s + 1) * 128], in_=trP[:].rearrange("p (k t) -> p k t", t=128)),
                    reads=[B["trP"]], writes=[B["xT"]])
            for s in range(NS):
                for gi in range(4):
                    pt, Bpt = tok_bank()
                    for kc in range(8):
                        sc.op("pe", lambda e, pt=pt, kc=kc, s=s, gi=gi: e.matmul(
                            pt[:], lhsT=xT[:, kc, s * 128:(s + 1) * 128],
                            rhs=w_tok[:, kc, gi * 512:(gi + 1) * 512], start=(kc == 0), stop=(kc == 7)),
                            reads=[B["xT"], B["w_tok"]], writes=[Bpt])
                    if gi == 0:
                        sc.op("act", lambda e, pt=pt, s=s: e.activation(out=uv[s][:], in_=pt[:],
                                                                          func=AF.Gelu_apprx_tanh),
                              reads=[Bpt], writes=[B["uv%d" % s]])
                    elif gi == 1:
                        sc.op("act", lambda e, pt=pt, s=s: e.activation(out=sgab[s][:], in_=pt[:],
                                                                          func=AF.Silu),
                              reads=[Bpt], writes=[B["sgab%d" % s]])
                    elif gi == 2:
                        sc.op("dve", lambda e, pt=pt, s=s: e.tensor_copy(out=vtok[s][:], in_=pt[:]),
                              reads=[Bpt], writes=[B["vtok%d" % s]])
                    else:
                        sc.op("act", lambda e, pt=pt, s=s: e.activation(out=sgc[s][:], in_=pt[:],
                                                                          func=AF.Silu),
                              reads=[Bpt], writes=[B["sgc%d" % s]])
            def feat_mm(c0):
                pt, Bpt = feat_bank()
                for kc in range(8):
                    sc.op("pe", lambda e, pt=pt, kc=kc, c0=c0: e.matmul(
                        pt[:, 0:T], lhsT=w_feat[:, kc, c0:c0 + 128], rhs=xT[:, kc, :],
                        start=(kc == 0), stop=(kc == 7)), reads=[B["xT"], B["w_feat"]], writes=[Bpt])
                return pt, Bpt
            for cc in range(2):
                pv, Bpv = feat_mm(cc * 128)
                pg, Bpg = feat_mm(256 + cc * 128)
                sc.op("act", lambda e, pg=pg: e.activation(out=thb[:], in_=pg[:, 0:T], func=AF.Tanh, scale=0.5),
                      reads=[Bpg], writes=[B["thb"]])
                sc.op("dve", lambda e, pv=pv, cc=cc: e.scalar_tensor_tensor(
                    out=hTc[:, cc, HALO:HALO + T], in0=thb[:], scalar=1.0, in1=pv[:, 0:T],
                    op0=ALU.add, op1=ALU.mult), reads=[B["thb"], Bpv], writes=[BhTc])
            for h in range(4):
                pq, Bpq = feat_mm(512 + h * 128)
                pf, Bpf = feat_mm(1024 + h * 128)
                sc.op("act", lambda e, pq=pq: e.activation(out=sq_t[:], in_=pq[:, 0:T], func=AF.Silu),
                      reads=[Bpq], writes=[B["sq_t"]])
                sc.op("act", lambda e, pf=pf: e.activation(out=thf[:], in_=pf[:, 0:T], func=AF.Tanh, scale=0.5),
                      reads=[Bpf], writes=[B["thf"]])
                sc.op("act", lambda e, h=h: e.activation(out=g_t[:], in_=thf[:], func=AF.Ln,
                                                         scale=hc_c[:, h:h + 1], bias=hc_a[:, h:h + 1]),
                      reads=[B["thf"], B["hc"]], writes=[B["g_t"]])
                sc.op("dve", lambda e, h=h: e.tensor_scalar(out=kk[:], in0=thf[:], scalar1=hc_nc[:, h:h + 1],
                                                            scalar2=hc_c[:, h:h + 1], op0=ALU.mult,
                                                            op1=ALU.add),
                      reads=[B["thf"], B["hc"]], writes=[B["kk"]])
                sc.op("dve", lambda e: e.tensor_tensor_scan(out=b_t[:], data0=rmask[:], data1=g_t[:],
                                                            initial=0.0, op0=ALU.mult, op1=ALU.add),
                      reads=[B["rmask"], B["g_t"]], writes=[B["b_t"]])
                sc.op("act", lambda e: e.activation(out=eb[:], in_=b_t[:], func=AF.Exp),
                      reads=[B["b_t"]], writes=[B["eb"]])
                sc.op("act", lambda e: e.activation(out=enb[:], in_=b_t[:], func=AF.Exp, scale=-1.0),
                      reads=[B["b_t"]], writes=[B["enb"]])
                sc.op("dve", lambda e, h=h: e.tensor_tensor(out=qT[:, h, :], in0=sq_t[:], in1=eb[:],
                                                            op=ALU.mult),
                      reads=[B["sq_t"], B["eb"]], writes=[B["qT"]])
                sc.op("dve", lambda e, h=h: e.tensor_tensor(out=kT[:, h, :], in0=kk[:], in1=enb[:],
                                                            op=ALU.mult),
                      reads=[B["kk"], B["enb"]], writes=[B["kT"]])
                sc.op("pool", lambda e, h=h: e.tensor_copy(
                    out=aj[:, h, :], in_=eb[:].rearrange("p (c j) -> p c j", j=64)[:, :, 63]),
                    reads=[B["eb"]], writes=[B["aj"]])

            for s in range(NS):
                sc.op("dve", lambda e, s=s: e.bn_stats(out=stats[:, 0:6], in_=uv[s][:, 256:512]),
                      reads=[B["uv%d" % s]], writes=[B["stats"]])
                sc.op("dve", lambda e: e.bn_aggr(out=mv[:], in_=stats[:, 0:6]), reads=[B["stats"]],
                      writes=[B["mv"]])
                rstd_from_var(mv[:, 1:2], rstd[:, 0:1], 1, [B["mv"]])
                sc.op("dve", lambda e, s=s: e.tensor_scalar(out=vn[:], in0=uv[s][:, 256:512],
                                                            scalar1=mv[:, 0:1], scalar2=rstd[:, 0:1],
                                                            op0=ALU.subtract, op1=ALU.mult),
                      reads=[B["uv%d" % s], B["mv"], B["rstd"]], writes=[B["vn"]])
                sc.op("pool", lambda e: e.tensor_tensor(out=vn[:], in0=vn[:], in1=bc_sgu_g[:], op=ALU.mult),
                      reads=[B["vn"], bc_bufs["bc_sgu_g"]], writes=[B["vn"]])
                sc.op("pool", lambda e: e.tensor_tensor(out=vbf[:], in0=vn[:], in1=bc_sgu_b[:], op=ALU.add),
                      reads=[B["vn"], bc_bufs["bc_sgu_b"]], writes=[B["vbf"]])
                sc.op("pool", lambda e, s=s: e.tensor_tensor(out=usg[:], in0=uv[s][:, 0:256],
                                                             in1=sgab[s][:, 0:256], op=ALU.mult),
                      reads=[B["uv%d" % s], B["sgab%d" % s]], writes=[B["usg"]])
                for h in range(4):
                    sc.op("pe", lambda e, h=h: e.matmul(mixP[:, h * 64:(h + 1) * 64], lhsT=WsT[:, h, :],
                                                        rhs=vbf[:, h * 64:(h + 1) * 64], start=True, stop=True),
                          reads=[B["WsT"], B["vbf"]], writes=[B["mixP"]])
                for h in range(4):
                    sc.op("dve", lambda e, h=h: e.scalar_tensor_tensor(
                        out=ybf[:, h * 64:(h + 1) * 64], in0=mixP[:, h * 64:(h + 1) * 64],
                        scalar=bs_t[:, h:h + 1], in1=usg[:, h * 64:(h + 1) * 64], op0=ALU.add, op1=ALU.mult),
                        reads=[B["mixP"], B["bs_t"], B["usg"]], writes=[B["ybf"]])
                for cc in range(2):
                    for k in range(CONV_W):
                        sc.op("pe", lambda e, cc=cc, k=k, s=s: e.matmul(
                            mixP[:, 256 + cc * 128:256 + (cc + 1) * 128],
                            lhsT=hTc[:, cc, s * 128 + k:s * 128 + k + 128], rhs=diag[:, cc, k, :],
                            start=(k == 0), stop=(k == CONV_W - 1)),
                            reads=[BhTc, B["diag"]], writes=[B["mixP"]])
                sc.op("dve", lambda e: e.tensor_tensor(out=cv[:], in0=mixP[:, 256:512], in1=bc_conv_b[:],
                                                       op=ALU.add),
                      reads=[B["mixP"], bc_bufs["bc_conv_b"]], writes=[B["cv"]])
                sc.op("dve", lambda e: e.bn_stats(out=stats[:, 0:6], in_=cv[:]), reads=[B["cv"]],
                      writes=[B["stats"]])
                sc.op("dve", lambda e: e.bn_aggr(out=mv[:], in_=stats[:, 0:6]), reads=[B["stats"]],
                      writes=[B["mv"]])
                rstd_from_var(mv[:, 1:2], rstd[:, 0:1], 1, [B["mv"]])
                sc.op("dve", lambda e: e.tensor_scalar(out=cn[:], in0=cv[:], scalar1=mv[:, 0:1],
                                                       scalar2=rstd[:, 0:1], op0=ALU.subtract, op1=ALU.mult),
                      reads=[B["cv"], B["mv"], B["rstd"]], writes=[B["cn"]])
                sc.op("pool", lambda e: e.tensor_tensor(out=cn[:], in0=cn[:], in1=bc_conv_g[:], op=ALU.mult),
                      reads=[B["cn"], bc_bufs["bc_conv_g"]], writes=[B["cn"]])
                sc.op("pool", lambda e: e.tensor_tensor(out=cn[:], in0=cn[:], in1=bc_conv_lb[:], op=ALU.add),
                      reads=[B["cn"], bc_bufs["bc_conv_lb"]], writes=[B["cn"]])
                sc.op("act", lambda e: e.activation(out=cs[:], in_=cn[:], func=AF.Silu), reads=[B["cn"]],
                      writes=[B["cs"]])
                sc.op("dve", lambda e, s=s: e.tensor_tensor(out=ybf[:, 256:512], in0=cs[:],
                                                            in1=sgab[s][:, 256:512], op=ALU.mult),
                      reads=[B["cs"], B["sgab%d" % s]], writes=[B["ybf"]])
                sl = slice(s * 128, (s + 1) * 128)
                for h in range(4):
                    sc.op("pe", lambda e, h=h, sl=sl: e.transpose(out=trP[:, h * 128:(h + 1) * 128],
                                                                  in_=kT[:, h, sl], identity=identb[:]),
                          reads=[B["kT"], B["identb"]], writes=[B["trP"]])
                for h in range(4):
                    sc.op("pe", lambda e, h=h, sl=sl: e.matmul(hgP[:, h * 128:(h + 1) * 128], lhsT=kT[:, h, sl],
                                                               rhs=qT[:, h, sl], start=True, stop=True),
                          reads=[B["kT"], B["qT"]], writes=[B["hgP"]])
                sc.op("act", lambda e: e.copy(out=ktok[:], in_=trP[:, 0:512]), reads=[B["trP"]],
                      writes=[B["ktok"]])
                sc.op("dve", lambda e: e.tensor_tensor(out=scm[:].rearrange("p h t -> p (h t)"), in0=hgP[:],
                                                       in1=mask4[:].rearrange("p h t -> p (h t)"), op=ALU.mult),
                      reads=[B["hgP"], B["mask4"]], writes=[B["scm"]])
                sc.op("pool", lambda e, s=s: e.tensor_tensor(out=ngs[:], in0=bc_ng[:], in1=sgc[s][:],
                                                             op=ALU.mult),
                      reads=[bc_bufs["bc_ng"], B["sgc%d" % s]], writes=[B["ngs"]])
                for h in range(4):
                    sc.op("pe", lambda e, h=h, s=s: e.matmul(
                        oP[:, h * 128:(h + 1) * 128], lhsT=scm[:, h, :], rhs=vtok[s][:, h * 128:(h + 1) * 128],
                        start=(h == 0), stop=False, skip_group_check=True),
                        reads=[B["scm"], B["vtok%d" % s]], writes=[B["oP"]])
                for c in range(2):
                    ch = s * 2 + c
                    rows = slice(c * 64, (c + 1) * 64)
                    cur = state["sbf"]
                    Scur, BScur = Sbf[cur], B["Sbf%d" % cur]
                    Snxt, BSnxt = Sbf[1 - cur], B["Sbf%d" % (1 - cur)]
                    state["sbf"] = 1 - cur
                    for h in range(4):
                        sc.op("pe", lambda e, h=h, rows=rows, s=s, c=c, Scur=Scur: e.matmul(
                            oP[rows, h * 128:(h + 1) * 128],
                            lhsT=qT[:, h, s * 128 + c * 64:s * 128 + (c + 1) * 64],
                            rhs=Scur[:, h * 128:(h + 1) * 128], start=False, stop=(c == 1 and h == 3),
                            skip_group_check=True),
                            reads=[B["qT"], BScur], writes=[B["oP"]])
                    pt, Bpt = feat_bank()
                    for h in range(4):
                        sc.op("pe", lambda e, h=h, rows=rows, s=s, pt=pt: e.matmul(
                            pt[:, h * 128:(h + 1) * 128], lhsT=ktok[rows, h * 128:(h + 1) * 128],
                            rhs=vtok[s][rows, h * 128:(h + 1) * 128], start=True, stop=True),
                            reads=[B["ktok"], B["vtok%d" % s]], writes=[Bpt])
                    for h in range(4):
                        sc.op("act", lambda e, h=h, pt=pt, ch=ch: e.activation(
                            out=Dp[:, h * 128:(h + 1) * 128], in_=pt[:, h * 128:(h + 1) * 128], func=AF.Copy,
                            scale=aj[:, h, ch:ch + 1]), reads=[Bpt, B["aj"]], writes=[B["Dp"]])
                    for h in range(4):
                        sc.op("dve", lambda e, h=h, ch=ch: e.scalar_tensor_tensor(
                            out=Sst[:, h * 128:(h + 1) * 128], in0=Sst[:, h * 128:(h + 1) * 128],
                            scalar=aj[:, h, ch:ch + 1], in1=Dp[:, h * 128:(h + 1) * 128],
                            op0=ALU.mult, op1=ALU.add), reads=[B["Sst"], B["aj"], B["Dp"]], writes=[B["Sst"]])
                    sc.op("pool", lambda e, Snxt=Snxt: e.tensor_copy(out=Snxt[:], in_=Sst[:]),
                          reads=[B["Sst"]], writes=[BSnxt])
                for h in range(4):
                    sc.op("act", lambda e, h=h: e.activation(out=junk[:], in_=oP[:, h * 128:(h + 1) * 128],
                                                             func=AF.Square, accum_out=ss[:, h:h + 1]),
                          reads=[B["oP"]], writes=[B["junk"], B["ss"]])
                rstd_from_var(ss[:], rstd[:, 0:4], 4, [B["ss"]], scale=1.0 / 128.0)
                for h in range(4):
                    sc.op("dve", lambda e, h=h: e.scalar_tensor_tensor(
                        out=ybf[:, 512 + h * 128:512 + (h + 1) * 128], in0=oP[:, h * 128:(h + 1) * 128],
                        scalar=rstd[:, h:h + 1], in1=ngs[:, h * 128:(h + 1) * 128], op0=ALU.mult, op1=ALU.mult),
                        reads=[B["oP"], B["rstd"], B["ngs"]], writes=[B["ybf"]])
                for kc in range(8):
                    sc.op("pe", lambda e, kc=kc: e.transpose(out=trP[:, kc * 128:(kc + 1) * 128],
                                                             in_=ybf[:, kc * 128:(kc + 1) * 128],
                                                             identity=identb[:]),
                          reads=[B["ybf"], B["identb"]], writes=[B["trP"]])
                sc.op("act", lambda e: e.copy(out=yT[:].rearrange("p k t -> p (k t)"), in_=trP[:]),
                      reads=[B["trP"]], writes=[B["yT"]])
                xi = xin[slot * NS + s]
                bx = Bxin[slot * NS + s]
                for half in range(2):
                    pt, Bpt = tok_bank()
                    for kc in range(8):
                        sc.op("pe", lambda e, pt=pt, kc=kc, half=half: e.matmul(
                            pt[:], lhsT=yT[:, kc, :], rhs=w_o[:, kc, half * 512:(half + 1) * 512],
                            start=(kc == 0), stop=(kc == 7)), reads=[B["yT"], B["w_o"]], writes=[Bpt])
                    sc.op("dve", lambda e, pt=pt, half=half, xi=xi: e.scalar_tensor_tensor(
                        out=xi[:, half * 512:(half + 1) * 512], in0=xi[:, half * 512:(half + 1) * 512],
                        scalar=float(ALPHA), in1=pt[:], op0=ALU.mult, op1=ALU.add),
                        reads=[bx, Bpt], writes=[bx])
                    sc.op("dve", lambda e, half=half, xi=xi: e.bn_stats(
                        out=stats[:, half * 6:(half + 1) * 6], in_=xi[:, half * 512:(half + 1) * 512]),
                        reads=[bx], writes=[B["stats"]])
                sc.op("dve", lambda e: e.bn_aggr(out=mv[:], in_=stats[:, 0:12]), reads=[B["stats"]],
                      writes=[B["mv"]])
                rstd_from_var(mv[:, 1:2], rstd[:, 0:1], 1, [B["mv"]])
                sc.op("dve", lambda e: e.scalar_tensor_tensor(out=nmr[:], in0=mv[:, 0:1], scalar=-1.0,
                                                              in1=rstd[:, 0:1], op0=ALU.mult, op1=ALU.mult),
                      reads=[B["mv"], B["rstd"]], writes=[B["nmr"]])
                sc.op("act", lambda e, xi=xi: e.activation(out=xi[:], in_=xi[:], func=AF.Identity,
                                                           scale=rstd[:, 0:1], bias=nmr[:, 0:1]),
                      reads=[bx, B["rstd"], B["nmr"]], writes=[bx])
                sc.op("pool", lambda e, xi=xi: e.tensor_tensor(out=xi[:], in0=xi[:], in1=bc_pg[:], op=ALU.mult),
                      reads=[bx, bc_bufs["bc_pg"]], writes=[bx])
                sc.op("pool", lambda e, xi=xi: e.tensor_tensor(out=xi[:], in0=xi[:], in1=bc_pb[:], op=ALU.add),
                      reads=[bx, bc_bufs["bc_pb"]], writes=[bx])
                r0 = ti * T + s * 128
                wr = [dst_bufs[r0 // 128]] if dst_bufs is not None else []
                sc.dma("pool", lambda e, xi=xi, r0=r0: e.dma_start(out=dst_d[r0:r0 + 128, :], in_=xi[:]), bx,
                       reads=[bx], writes=wr)

        gslot = 0
        for li, l in enumerate(layers):
            src_d = x_d if li == 0 else scratch_d
            src_bufs = None if li == 0 else Bmid
            dst_d = out_d if li == L - 1 else scratch_d
            dst_bufs = None if li == L - 1 else Bmid
            layer_setup(l)
            load_tile(src_d, 0, gslot % 2, src_bufs)
            for ti in range(NT):
                if ti + 1 < NT:
                    load_tile(src_d, ti + 1, (gslot + 1) % 2, src_bufs)
                tile_body(l, ti, gslot % 2, dst_d, dst_bufs)
                gslot += 1
        sc.emit(final_wait_bufs=Bxin)
    return nc


_PARAM_NAMES = ["w_in", "sgu_ln_g", "sgu_ln_b", "sgu_w", "sgu_b", "conv_w", "conv_b", "conv_ln_g",
                "conv_ln_b", "hgrn_lb", "hgrn_norm_g", "w_out", "post_ln_g", "post_ln_b"]

_PROG_CACHE = {}


def _get_prog(S, layers, T=256):
    key = (S, tuple(layers), T)
    if key not in _PROG_CACHE:
        _PROG_CACHE[key] = build_program(S, list(layers), T)
    return _PROG_CACHE[key]


def run_layers(x, params, layers, n_cores=BATCH, S=SEQ):
    nc = _get_prog(S, layers)
    in_maps = []
    for b in range(n_cores):
        m = {"x": np.ascontiguousarray(x[b, :S])}
        for n in _PARAM_NAMES:
            m[n] = params[n]
        in_maps.append(m)
    res = run_bass_kernel_spmd(nc, in_maps, core_ids=list(range(n_cores)))
    return np.stack([np.asarray(r["out"]) for r in res.results], axis=0)


def kernel(**inputs):
    x = np.asarray(inputs["x"], dtype=np.float32)
    params = {n: np.ascontiguousarray(np.asarray(inputs[n], dtype=np.float32)) for n in _PARAM_NAMES}
    out = run_layers(x, params, list(range(DEPTH)))
    return out.astype(np.float32)
```
